# Optimizing a Trainium2 kernel written in Bass

```python
import jax, jax.numpy as jnp
from jax import lax
import numpy as np

D_MODEL = 4096
BATCH = 2
SEQ = 8192
DEPTH = 4

CTX_LEN = 256
GRID_W = 64
N_BRANCH = 4
W_BR = D_MODEL // N_BRANCH
W_A = W_BR
W_B = W_BR
W_C = W_BR
W_D = W_BR
LRU_HEADS = 16
LRU_HEAD_DIM = W_A // LRU_HEADS
LRU_C = 8.0
LRU_CONV = 4
CF_CONV = 31
SC_CONV = 3
FNET_GROUPS = 4
EPS = 1e-6
N_IN = 2 * W_A + 3 * W_B + 4 * W_C + 2 * W_D + N_BRANCH * D_MODEL

kernel_name = "hybrid_lru_conformer_shortconv_fnet_diffusion_trunk"


def _split_points():
    sizes = (W_A, 2 * W_B, 3 * W_C, W_D, W_A, W_B, W_C, W_D) + (D_MODEL,) * N_BRANCH
    pts, s = [], 0
    for n in sizes[:-1]:
        s += n
        pts.append(s)
    return pts


def _rmsnorm(v, g):
    v32 = v.astype(jnp.float32)
    n = v32 * lax.rsqrt(jnp.mean(v32 * v32, axis=-1, keepdims=True) + EPS)
    return (n * g.astype(jnp.float32)).astype(v.dtype)


def _layernorm(v, g, b):
    v32 = v.astype(jnp.float32)
    mu = jnp.mean(v32, axis=-1, keepdims=True)
    d = v32 - mu
    var = jnp.mean(d * d, axis=-1, keepdims=True)
    return (d * lax.rsqrt(var + EPS) * g.astype(jnp.float32) + b.astype(jnp.float32)).astype(v.dtype)


def _dwconv(v, w, pad_l, pad_r):
    return lax.conv_general_dilated(
        v, w.astype(v.dtype)[:, None, :], window_strides=(1,), padding=[(pad_l, pad_r)],
        dimension_numbers=("NWC", "WIO", "NWC"), feature_group_count=v.shape[-1])


def _lru_combine(earlier, later):
    a_e, b_e = earlier
    a_l, b_l = later
    return a_e * a_l, a_l * b_e + b_l


def _rglru_scan(u, wa, ba, wx, bx, lam, h0, reverse):
    bsz, n, w = u.shape
    uh = u.reshape(bsz, n, LRU_HEADS, LRU_HEAD_DIM)
    r = jax.nn.sigmoid((jnp.einsum("bnhd,hde->bnhe", uh, wa).reshape(bsz, n, w) + ba).astype(jnp.float32))
    i = jax.nn.sigmoid((jnp.einsum("bnhd,hde->bnhe", uh, wx).reshape(bsz, n, w) + bx).astype(jnp.float32))
    log_a = LRU_C * r * jax.nn.log_sigmoid(lam.astype(jnp.float32))
    a = jnp.exp(log_a)
    b = jnp.sqrt(-jnp.expm1(2.0 * log_a)) * (i * u.astype(jnp.float32))
    if h0 is not None:
        edge = -1 if reverse else 0
        b = b.at[:, edge].add(a[:, edge] * h0)
    _, h = lax.associative_scan(_lru_combine, (a, b), axis=1, reverse=reverse)
    return h


def _rglru_mixer(xa, p, h0_f, h0_b):
    u_f = _dwconv(xa, p["lru_conv_w"][0], LRU_CONV - 1, 0) + p["lru_conv_b"][0]
    u_b = _dwconv(xa, p["lru_conv_w"][1], 0, LRU_CONV - 1) + p["lru_conv_b"][1]
    h_f = _rglru_scan(u_f, p["lru_wa"][0], p["lru_ba"][0], p["lru_wx"][0], p["lru_bx"][0],
                      p["lru_lambda"][0], h0_f, False)
    h_b = _rglru_scan(u_b, p["lru_wa"][1], p["lru_ba"][1], p["lru_wx"][1], p["lru_bx"][1],
                      p["lru_lambda"][1], h0_b, True)
    return h_f, h_b


def _conformer_conv(zb, p):
    val, gl = jnp.split(zb, 2, axis=-1)
    v = val * jax.nn.sigmoid(gl)
    v = _dwconv(v, p["cf_conv_w"], CF_CONV // 2, CF_CONV // 2) + p["cf_conv_b"]
    return jax.nn.silu(_layernorm(v, p["cf_ln_g"], p["cf_ln_b"]))


def _short_conv(zc, p):
    xin, gate_b, gate_c = jnp.split(zc, 3, axis=-1)
    return gate_b * _dwconv(gate_c * xin, p["sc_conv_w"], SC_CONV // 2, SC_CONV // 2)


def _fourier_mix(zd):
    bsz, n, w = zd.shape
    v = zd.astype(jnp.float32).reshape(bsz, n, FNET_GROUPS, w // FNET_GROUPS)
    f = jnp.fft.fftn(v, axes=(1, 3), norm="ortho").real
    return f.reshape(bsz, n, w).astype(zd.dtype)


def _token_mix(z, y_a, p):
    (_, zb, zc, zd, s_a, s_b, s_c, s_d, g_a, g_b, g_c, g_d) = jnp.split(z, _split_points(), axis=-1)
    ys = (y_a, _conformer_conv(zb, p), _short_conv(zc, p), _fourier_mix(zd))
    ss = (s_a, s_b, s_c, s_d)
    gs = (g_a, g_b, g_c, g_d)
    merged = None
    for k in range(N_BRANCH):
        br = (ys[k] * jax.nn.silu(ss[k])) @ p["w_branch"][k]
        term = jax.nn.sigmoid(gs[k]) * br
        merged = term if merged is None else merged + term
    return merged @ p["w_out"]


def setup_inputs(seed: int = 0) -> dict:
    key = jax.random.key(seed)
    ks = jax.random.split(key, 24)
    f32 = jnp.float32

    def nrm(k, shape, scale):
        return jax.random.normal(k, shape, f32) * scale

    a0 = jax.random.uniform(ks[14], (DEPTH, 2, W_A), f32, minval=0.9, maxval=0.999)
    return {
        "x": nrm(ks[0], (BATCH, SEQ, D_MODEL), 1.0),
        "c": nrm(ks[1], (BATCH, D_MODEL), 1.0),
        "ctx": nrm(ks[2], (BATCH, CTX_LEN, D_MODEL), 1.0),
        "c_ctx": nrm(ks[3], (D_MODEL,), 1.0),
        "norm_g": 1.0 + nrm(ks[4], (DEPTH, D_MODEL), 0.02),
        "w_mod": nrm(ks[5], (DEPTH, D_MODEL, 3 * D_MODEL), 0.5 * D_MODEL ** -0.5),
        "b_mod": nrm(ks[6], (DEPTH, 3 * D_MODEL), 0.01),
        "w_in": nrm(ks[7], (DEPTH, D_MODEL, N_IN), D_MODEL ** -0.5),
        "lru_conv_w": nrm(ks[8], (DEPTH, 2, LRU_CONV, W_A), LRU_CONV ** -0.5),
        "lru_conv_b": nrm(ks[9], (DEPTH, 2, W_A), 0.01),
        "lru_wa": nrm(ks[10], (DEPTH, 2, LRU_HEADS, LRU_HEAD_DIM, LRU_HEAD_DIM), LRU_HEAD_DIM ** -0.5),
        "lru_ba": nrm(ks[11], (DEPTH, 2, W_A), 0.01),
        "lru_wx": nrm(ks[12], (DEPTH, 2, LRU_HEADS, LRU_HEAD_DIM, LRU_HEAD_DIM), LRU_HEAD_DIM ** -0.5),
        "lru_bx": nrm(ks[13], (DEPTH, 2, W_A), 0.01),
        "lru_lambda": jnp.log(a0) - jnp.log1p(-a0),
        "cf_conv_w": nrm(ks[15], (DEPTH, CF_CONV, W_B), CF_CONV ** -0.5),
        "cf_conv_b": nrm(ks[16], (DEPTH, W_B), 0.01),
        "cf_ln_g": 1.0 + nrm(ks[17], (DEPTH, W_B), 0.02),
        "cf_ln_b": nrm(ks[18], (DEPTH, W_B), 0.01),
        "sc_conv_w": nrm(ks[19], (DEPTH, SC_CONV, W_C), SC_CONV ** -0.5),
        "w_branch": nrm(ks[20], (DEPTH, N_BRANCH, W_BR, D_MODEL), W_BR ** -0.5),
        "w_out": nrm(ks[21], (DEPTH, D_MODEL, D_MODEL), D_MODEL ** -0.5),
        "final_g": 1.0 + nrm(ks[22], (D_MODEL,), 0.02),
    }


def reference(x, c, ctx, c_ctx, norm_g, w_mod, b_mod, w_in, lru_conv_w, lru_conv_b, lru_wa, lru_ba,
              lru_wx, lru_bx, lru_lambda, cf_conv_w, cf_conv_b, cf_ln_g, cf_ln_b, sc_conv_w,
              w_branch, w_out, final_g):
    silu_c = jax.nn.silu(c)
    silu_cc = jax.nn.silu(c_ctx)
    xc = ctx
    for l in range(DEPTH):
        last = l == DEPTH - 1
        p = {
            "lru_conv_w": lru_conv_w[l], "lru_conv_b": lru_conv_b[l],
            "lru_wa": lru_wa[l], "lru_ba": lru_ba[l], "lru_wx": lru_wx[l], "lru_bx": lru_bx[l],
            "lru_lambda": lru_lambda[l],
            "cf_conv_w": cf_conv_w[l], "cf_conv_b": cf_conv_b[l],
            "cf_ln_g": cf_ln_g[l], "cf_ln_b": cf_ln_b[l],
            "sc_conv_w": sc_conv_w[l], "w_branch": w_branch[l], "w_out": w_out[l],
        }
        mod = silu_c @ w_mod[l] + b_mod[l]
        shift, scale, gate = jnp.split(mod, 3, axis=-1)
        h = _rmsnorm(x, norm_g[l]) * (1.0 + scale[:, None]) + shift[:, None]
        z = h @ w_in[l]
        n_mod_c = 2 * D_MODEL if last else 3 * D_MODEL
        mod_c = silu_cc @ w_mod[l][:, :n_mod_c] + b_mod[l][:n_mod_c]
        hc = _rmsnorm(xc, norm_g[l]) * (1.0 + mod_c[D_MODEL:2 * D_MODEL]) + mod_c[:D_MODEL]
        n_in_c = W_A if last else N_IN
        zc = hc @ w_in[l][:, :n_in_c]
        hc_f, hc_b = _rglru_mixer(zc[..., :W_A], p, None, None)
        hl_f, hl_b = _rglru_mixer(z[..., :W_A], p, hc_f[:, -1], hc_b[:, 0])
        out = _token_mix(z, (hl_f + hl_b).astype(z.dtype), p)
        if not last:
            out_c = _token_mix(zc, (hc_f + hc_b).astype(zc.dtype), p)
            xc = xc + mod_c[2 * D_MODEL:] * out_c
        x = x + gate[:, None] * out
    return _rmsnorm(x, final_g)
```

```python
from contextlib import ExitStack
import numpy as np
import concourse.bass as bass
import concourse.mybir as mybir
from concourse.bass_utils import run_bass_kernel_spmd

F32 = mybir.dt.float32
BF16 = mybir.dt.bfloat16
AF = mybir.ActivationFunctionType
ALU = mybir.AluOpType
EPS = 1e-6
PAD = 16
NR = 20
import os
KSTOP = float(os.environ.get('KSTOP', '99'))


class Cfg:
    def __init__(s, D, SEQ, CTX, DEPTH):
        s.D, s.SEQ, s.CTX, s.DEPTH = D, SEQ, CTX, DEPTH
        s.W = D // 4
        s.CH = s.W // 4
        s.NCH = s.CH // 128
        s.KC = D // 128
        s.WC = s.W // 128
        s.TL = SEQ // 4
        s.TC = CTX // 4
        s.T = s.TL + s.TC
        s.T1 = SEQ // 128
        s.NTC = CTX // 128
        s.N_IN = 11 * s.W + 4 * D
        s.WCOLS = 3 * D + s.N_IN + 2 * D
        s.MOD0 = 0
        s.IN0 = 3 * D
        s.BR0 = s.IN0 + s.N_IN
        s.OUT0 = s.BR0 + D
        s.RS = D // 64
        s.P0 = PAD
        s.P1 = 2 * PAD + CTX
        s.LB = 3 * PAD + CTX + SEQ
        tiles = []
        o = 0
        while o < s.TL:
            n = min(512, s.TL - o)
            tiles.append((o, n, 0))
            o += n
        tiles.append((s.TL, s.TC, 1))
        s.tiles = tiles
        nt = len(tiles)
        hcut = max(1, (nt - 1) // 2) if nt > 2 else 1
        s.halves = [list(range(0, hcut)), list(range(hcut, nt))]
        s.hbase = [tiles[h[0]][0] for h in s.halves]
        s.TA = max(sum(tiles[ti][1] for ti in h) for h in s.halves)
        off = {}
        c = 0

        def add(name, n):
            nonlocal c
            off[name] = c
            c += n
        add("bmod", 3 * s.KC)
        add("normg", s.KC)
        add("lng", s.WC)
        add("lnb", s.WC)
        add("lcw", s.NCH * 8)
        add("lcb", s.NCH * 2)
        add("lba", s.NCH * 2)
        add("lbx", s.NCH * 2)
        add("lam", s.NCH * 2)
        add("cfw", s.NCH * 31)
        add("cfb", s.NCH)
        add("scw", s.NCH * 3)
        s.po = off
        s.NP = c
        co = {}
        c = 0

        def addc(name, n):
            nonlocal c
            co[name] = c
            c += n
        addc("cs", s.NCH * 2 * s.CH)
        addc("l1", 2 * s.T1)
        addc("l2", 2 * s.T1)
        addc("twc", 128)
        addc("tws", 128)
        addc("c128", 128)
        addc("s128", 128)
        addc("cx", s.NTC * 2 * CTX)
        addc("sel", 8)
        addc("fing", s.KC)
        addc("cvec", s.KC * 2)
        s.co = co
        s.NCST = c


KLOG = []


class Buf:
    __slots__ = ("w", "r")

    def __init__(s):
        s.w = None
        s.r = {}


class KB:
    def __init__(s, nc, es):
        s.nc, s.es = nc, es
        s.engs = {"pe": nc.tensor, "act": nc.scalar, "dve": nc.vector, "pool": nc.gpsimd, "sp": nc.sync}
        s.sems = {}
        for e in s.engs:
            s.sems[("e", e)] = es.enter_context(nc.semaphore("e_" + e))
        s.ecnt = {e: 0 for e in s.engs}
        s.rval = {}
        s.rpos = {}
        for q in ("sp", "pool"):
            for i in range(NR):
                s.sems[("d", q, i)] = es.enter_context(nc.semaphore(f"d_{q}{i}"))
            s.rval[q] = [0] * NR
            s.rpos[q] = 0
        s.sems[("c",)] = es.enter_context(nc.semaphore("coll"))
        s.ccnt = 0
        s.sems[("b",)] = es.enter_context(nc.semaphore("bar"))
        s.bcnt = 0
        s.waited = {e: {} for e in s.engs}
        s.bufs = {}

    def B(s, key):
        b = s.bufs.get(key)
        if b is None:
            b = s.bufs[key] = Buf()
        return b

    def _wait1(s, E, tok):
        key, val = tok
        if key == ("e", "pe") and E == "pe":
            return
        if s.waited[E].get(key, 0) >= val:
            return
        s.engs[E].wait_ge(s.sems[key], val)
        KLOG.append(("wait", E, key, val))
        s.waited[E][key] = val

    def _deps(s, E, reads, writes):
        for b in reads:
            if b.w is not None:
                s._wait1(E, b.w)
        for b in writes:
            if b.w is not None:
                s._wait1(E, b.w)
            for t in b.r.values():
                s._wait1(E, t)

    def _record(s, tok, reads, writes):
        for b in reads:
            b.r[tok[0]] = tok
        for b in writes:
            b.w = tok
            b.r = {}

    def op(s, E, fn, reads=(), writes=()):
        s._deps(E, reads, writes)
        ins = fn(s.engs[E])
        s.ecnt[E] += 1
        KLOG.append(("op", E, s.ecnt[E]))
        ins.then_inc(s.sems[("e", E)], 1)
        s._record((("e", E), s.ecnt[E]), reads, writes)

    def grp(s, E, fns, reads=(), writes=()):
        s._deps(E, reads, writes)
        ins = None
        for fn in fns:
            ins = fn(s.engs[E])
        s.ecnt[E] += 1
        KLOG.append(("grp", E, s.ecnt[E], len(fns)))
        KLOG.append(("op", E, s.ecnt[E]))
        ins.then_inc(s.sems[("e", E)], 1)
        s._record((("e", E), s.ecnt[E]), reads, writes)

    def mm(s, ps_buf, items, reads):
        pbs = ps_buf if isinstance(ps_buf, (list, tuple)) else [ps_buf]
        s._deps("pe", reads, pbs)
        ins = None
        for (o, l, r, st, sp_) in items:
            ins = s.nc.tensor.matmul(o, l, r, start=st, stop=sp_)
        s.ecnt["pe"] += 1
        KLOG.append(("mm", len(items), s.ecnt["pe"]))
        ins.then_inc(s.sems[("e", "pe")], 1)
        s._record((("e", "pe"), s.ecnt["pe"]), reads, pbs)

    def dma(s, q, out, in_, reads=(), writes=()):
        i = s.rpos[q]
        s.rpos[q] = (i + 1) % NR
        key = ("d", q, i)
        if s.rval[q][i] > 0:
            s._wait1(q, (key, s.rval[q][i]))
        s._deps(q, reads, writes)
        ins = s.engs[q].dma_start(out=out, in_=in_)
        s.rval[q][i] += 16
        KLOG.append(("dma", q, i, s.rval[q][i]))
        ins.then_inc(s.sems[key], 16)
        s._record((key, s.rval[q][i]), reads, writes)

    def coll(s, ins_ap, out_ap, groups, reads=(), writes=()):
        s._deps("pool", reads, writes)
        ins = s.nc.gpsimd.collective_compute("AllGather", ALU.bypass, replica_groups=groups,
                                             ins=[ins_ap.opt()], outs=[out_ap.opt()])
        s.ccnt += 1
        ins.then_inc(s.sems[("c",)])
        s._record((("c",), s.ccnt), reads, writes)

    def barrier(s, scratch):
        for e in s.engs:
            if s.ecnt[e] > 0:
                s._wait1("pool", (("e", e), s.ecnt[e]))
        for q in s.rval:
            for i in range(NR):
                if s.rval[q][i] > 0:
                    s._wait1("pool", (("d", q, i), s.rval[q][i]))
        if s.ccnt > 0:
            s._wait1("pool", (("c",), s.ccnt))
        s.bcnt += 1
        s.nc.gpsimd.memset(scratch, 0.0).then_inc(s.sems[("b",)], 1)
        for e in s.engs:
            if e != "pool":
                s.engs[e].wait_ge(s.sems[("b",)], s.bcnt)
        for e in s.engs:
            for k in s.sems:
                if k[0] == "e":
                    s.waited[e][k] = s.ecnt[k[1]]
                elif k[0] == "d":
                    s.waited[e][k] = s.rval[k[1]][k[2]]
                elif k[0] == "c":
                    s.waited[e][k] = s.ccnt


def build(cfg):
    c = cfg
    nc = bass.Bass("TRN2", target_bir_lowering=False)
    D, W, CH, NCH, KC, WC, T, TL, TC = c.D, c.W, c.CH, c.NCH, c.KC, c.WC, c.T, c.TL, c.TC
    SEQ, CTX, T1, NTC, LB, P0, P1 = c.SEQ, c.CTX, c.T1, c.NTC, c.LB, c.P0, c.P1
    xT_in = nc.dram_tensor("xT", [D, T], F32, kind="ExternalInput").ap()
    wsl = nc.dram_tensor("wsl", [c.DEPTH * 8 * c.RS, c.WCOLS], F32, kind="ExternalInput").ap()
    pp_in = nc.dram_tensor("pp", [c.DEPTH, 128, c.NP], F32, kind="ExternalInput").ap()
    gm_in = nc.dram_tensor("gm", [c.DEPTH, 128, 4 * NCH * 128], F32, kind="ExternalInput").ap()
    cst_in = nc.dram_tensor("cst", [128, c.NCST], F32, kind="ExternalInput").ap()
    yT = nc.dram_tensor("yT", [D, TL], F32, kind="ExternalOutput").ap()
    XT = nc.dram_tensor("XTs", [D, T], F32).ap()
    WF = [[nc.dram_tensor(f"WF{l}_{i}", [8 * c.RS, c.WCOLS], F32).ap() for i in range(8)] for l in range(c.DEPTH)]
    WB = [nc.dram_tensor(f"WBn{i}", [c.RS, c.WCOLS], F32).ap() for i in range(c.DEPTH * 8)]
    MIXIN = [nc.dram_tensor(f"MIXIN{h}", [8 * CH, T], F32).ap() for h in range(2)]
    MIXG = [nc.dram_tensor(f"MIXG{h}", [64 * CH, T], F32).ap() for h in range(2)]
    MOUT = [nc.dram_tensor(f"MOUT{h}", [8 * CH, T], F32).ap() for h in range(2)]
    MOUTG = [nc.dram_tensor(f"MOUTG{h}", [64 * CH, T], F32).ap() for h in range(2)]
    GBd = nc.dram_tensor("GBd", [W, T], F32).ap()
    Sd = nc.dram_tensor("Sd", [4 * W, T], F32).ap()
    Gd_ = nc.dram_tensor("Gd", [4 * D, T], F32).ap()
    MT = nc.dram_tensor("MT", [D, T], BF16).ap()
    FG = nc.dram_tensor("FG", [2, T1, 128, CH], F32).ap()
    FZ = nc.dram_tensor("FZ", [2, T1, 128, CH], F32).ap()
    grp4 = [[0, 1, 2, 3], [4, 5, 6, 7]]
    grp8 = [list(range(8))]

    with ExitStack() as es:
        kb = KB(nc, es)
        B = kb.B

        uid = [0]

        def sb(name, shape, dt=F32, stack=None):
            uid[0] += 1
            return (stack or es).enter_context(nc.sbuf_tensor(f"sb_{name}_{uid[0]}", shape, dt))
        ones = sb("ones", [128, 128])
        onesW = sb("onesW", [128, 128])
        epsc = sb("epsc", [128, 1])
        cst = sb("cst", [128, c.NCST])
        pp = sb("pp", [128, c.NP])
        gm = sb("gm", [128, 4 * NCH * 128])
        modT = sb("modT", [128, 3 * KC, 2])
        Amod = sb("Amod", [128, KC, 2])
        tmpm = sb("tmpm", [128, KC, 2])
        scv = sb("scv", [128, KC, 2], BF16)
        c1t = sb("c1t", [128, NCH * 2, 2])
        scr = sb("scr", [128, 4])
        PS = [es.enter_context(nc.psum_tensor(f"ps{i}", [128, 512], F32)) for i in range(8)]
        psi = [0]

        def nps():
            i = psi[0]
            psi[0] = (i + 1) % 8
            return PS[i], B(("ps", i))

        def co(name):
            return c.co[name]

        def po(name):
            return c.po[name]
        sel = cst[:, co("sel"):co("sel") + 8]

        kb.op("dve", lambda e: e.memset(ones[:], 1.0 / D), writes=[B("ones")])
        kb.op("dve", lambda e: e.memset(onesW[:], 1.0 / W), writes=[B("ones")])
        kb.op("dve", lambda e: e.memset(epsc[:], EPS), writes=[B("ones")])
        kb.dma("sp", cst[:], cst_in, writes=[B("cst")])
        for kk in range(KC):
            kb.dma("sp", XT[kk * 128:(kk + 1) * 128, :], xT_in[kk * 128:(kk + 1) * 128, :], writes=[B(("XT", ti)) for ti in range(len(c.tiles))])
        for l in range(c.DEPTH):
            for sidx in range(8):
                i = l * 8 + sidx
                for h in range(0, c.RS, 8):
                    kb.dma("sp", WB[i][h:h + 8, :].rearrange("r (a b) -> (r a) b", b=512),
                           wsl[i * c.RS + h:i * c.RS + h + 8, :].rearrange("r (a b) -> (r a) b", b=512), writes=[B(("WB", i))])
                kb.coll(WB[i], WF[l][sidx], grp8, reads=[B(("WB", i))], writes=[B(("WF", l))])
        kb.op("act", lambda e: e.activation(out=scv[:], in_=cst[:, co("cvec"):co("cvec") + 2 * KC].rearrange("p (k v) -> p k v", v=2), func=AF.Silu),
              reads=[B("cst")], writes=[B("scv")])

        if KSTOP <= 1:
            kb.barrier(scr[:, 0:1])
            return nc
        def run_gemm(l, jobs, act_ap_fn, act_bufs, tiles, wslots, wbufs, stg):
            wi = [0]
            gi_ = [0]
            NW = len(wslots)
            KPS = KC // 8

            def load(chunks):
                slots = []
                for (col, k0, k1) in chunks:
                    j = wi[0]
                    wi[0] = (j + 1) % NW
                    g = gi_[0]
                    gi_[0] = (g + 1) % len(stg)
                    parts = []
                    for sx in range(k0 // KPS, (k1 - 1) // KPS + 1):
                        ka, kb_ = max(k0, sx * KPS), min(k1, (sx + 1) * KPS)
                        pb = B(("stg", g, sx - k0 // KPS))
                        kb.dma("sp", stg[g][:, ka - k0:kb_ - k0, :],
                               WF[l][sx].rearrange("(kc p) c -> p kc c", p=128)[:, ka - sx * KPS:kb_ - sx * KPS, col:col + 128],
                               reads=[B(("WF", l))], writes=[pb])
                        parts.append(pb)
                    kb.op("pool", lambda e: e.tensor_copy(out=wslots[j][:, 0:k1 - k0, :], in_=stg[g][:, 0:k1 - k0, :]), reads=parts, writes=[wbufs[j]])
                    slots.append(j)
                return slots
            pending = load(jobs[0][0]) if jobs else None
            for ji, (chunks, epi) in enumerate(jobs):
                slots = pending
                pending = load(jobs[ji + 1][0]) if ji + 1 < len(jobs) else None
                for (ti, pos, (off, n, v)) in tiles:
                    pss = []
                    for ci, (col, k0, k1) in enumerate(chunks):
                        ps, psb = nps()
                        j = slots[ci]
                        items = []
                        for kk in range(k0, k1):
                            items.append((ps[:, 0:n], wslots[j][:, kk - k0, :], act_ap_fn(kk, off, n), kk == k0, kk == k1 - 1))
                        kb.mm(psb, items, reads=[wbufs[j], act_bufs[pos]])
                        pss.append((ps, psb))
                    epi(ti, (off, n, v), pss)

        for l in range(c.DEPTH):
            kb.dma("sp", pp[:], pp_in[l], writes=[B("pp")])
            kb.dma("sp", gm[:], gm_in[l], writes=[B("gm")])
            with ExitStack() as ph:
                act = sb("act", [128, KC, c.TA], BF16, ph)
                NWS = 5
                wbf = [B(("w", i)) for i in range(NWS)]
                wsl_ = [sb(f"w{i}", [128, KC, 128], BF16, ph) for i in range(NWS)]
                stg = [sb(f"stg{i}", [128, KC, 128], F32, ph) for i in range(2)]
                TMP = [sb(f"tmp{i}", [128, 512], F32, ph) for i in range(6)]
                tpi = [0]

                def ntmp():
                    i = tpi[0]
                    tpi[0] = (i + 1) % 6
                    return TMP[i], B(("tmp", i))
                NS = 128
                xt = sb("xt", [128, KC, NS], F32, ph)
                sqb = sb("sqb", [128, KC, NS], F32, ph)
                rst = sb("rst", [128, NS], F32, ph)
                actb = [B(("act", pos)) for pos in range(len(c.tiles))]

                def epi_mod(m):
                    def f(ti, tile, pss):
                        ps, psb = pss[0]
                        kb.op("act", lambda e: e.activation(out=modT[:, m, :], in_=ps[:, 0:2], func=AF.Identity,
                                                             bias=pp[:, po("bmod") + m:po("bmod") + m + 1]),
                              reads=[psb, B("pp")], writes=[B("modT")])
                    return f
                jobs = [([(c.MOD0 + m * 128, 0, KC)], epi_mod(m)) for m in range(3 * KC)]
                run_gemm(l, jobs, lambda kk, off, n: scv[:, kk, 0:2], [B("scv")], [(0, 0, (0, 2, 0))], wsl_, wbf, stg)
                if KSTOP <= 2:
                    kb.barrier(scr[:, 0:1])
                    return nc
                kb.op("dve", lambda e: e.tensor_scalar(out=tmpm[:], in0=modT[:, KC:2 * KC, :], scalar1=1.0, scalar2=None, op0=ALU.add),
                      reads=[B("modT")], writes=[B("tmpm")])
                for v in range(2):
                    kb.op("dve", lambda e: e.tensor_tensor(out=Amod[:, :, v], in0=tmpm[:, :, v], in1=pp[:, po("normg"):po("normg") + KC], op=ALU.mult),
                          reads=[B("tmpm"), B("pp")], writes=[B("Amod")])
                def hbuild(tl, hb):
                    for (ti, pos, (off, n, v)) in tl:
                        for so in range(0, n, NS):
                            ns = min(NS, n - so)
                            o0 = off + so
                            for hs in range(0, KC, 8):
                                kb.dma("sp", xt[:, hs:hs + 8, 0:ns], XT.rearrange("(kc p) t -> p kc t", p=128)[:, hs:hs + 8, o0:o0 + ns],
                                       reads=[B(("XT", ti))], writes=[B("xt")])
                            ps, psb = nps()
                            kb.op("act", lambda e: e.activation(out=sqb[:, :, 0:ns], in_=xt[:, :, 0:ns], func=AF.Square), reads=[B("xt")], writes=[B("sqb")])
                            kb.mm(psb, [(ps[:, 0:ns], ones[:], sqb[:, kk, 0:ns], kk == 0, kk == KC - 1) for kk in range(KC)], reads=[B("sqb"), B("ones")])
                            kb.op("act", lambda e: e.activation(out=rst[:, 0:ns], in_=ps[:, 0:ns], func=AF.Sqrt, bias=epsc[:, 0:1]), reads=[psb, B("ones")], writes=[B("rst")])
                            kb.op("dve", lambda e: e.reciprocal(out=rst[:, 0:ns], in_=rst[:, 0:ns]), reads=[B("rst")], writes=[B("rst")])
                            kb.grp("dve", [(lambda e, kk=kk: e.scalar_tensor_tensor(out=sqb[:, kk, 0:ns], in0=xt[:, kk, 0:ns], scalar=Amod[:, kk, v:v + 1],
                                                                                     in1=rst[:, 0:ns], op0=ALU.mult, op1=ALU.mult)) for kk in range(KC)],
                                   reads=[B("xt"), B("rst"), B("Amod")], writes=[B("sqb")])
                            kb.grp("act", [(lambda e, kk=kk: e.activation(out=act[:, kk, o0 - hb:o0 - hb + ns], in_=sqb[:, kk, 0:ns], func=AF.Identity,
                                                                           bias=modT[:, kk, v:v + 1])) for kk in range(KC)],
                                   reads=[B("sqb"), B("modT")], writes=[actb[pos]])

                def mrow(q, br):
                    ch0 = q * 128
                    j = ch0 // CH
                    return (j // 2, (j % 2) * 4 * CH + br * CH + (ch0 % CH))

                def store(dst, r0, tile, src_t, src_b, key):
                    off, n, v = tile
                    kb.dma("sp", dst[r0:r0 + 128, off:off + n], src_t[:, 0:n], reads=[src_b], writes=[B(key)])

                def epi_copy(dst, r0, key, func=AF.Copy):
                    def f(ti, tile, pss):
                        ps, psb = pss[0]
                        n = tile[1]
                        t, tb = ntmp()
                        kb.op("act", lambda e: e.activation(out=t[:, 0:n], in_=ps[:, 0:n], func=func), reads=[psb], writes=[tb])
                        store(dst, r0, tile, t, tb, (key, r0, ti))
                    return f

                def epi_B(q):
                    def f(ti, tile, pss):
                        (pv, pvb), (pg, pgb) = pss
                        n = tile[1]
                        t, tb = ntmp()
                        kb.op("act", lambda e: e.activation(out=t[:, 0:n], in_=pg[:, 0:n], func=AF.Sigmoid), reads=[pgb], writes=[tb])
                        t2, t2b = ntmp()
                        kb.op("dve", lambda e: e.tensor_tensor(out=t2[:, 0:n], in0=pv[:, 0:n], in1=t[:, 0:n], op=ALU.mult), reads=[pvb, tb], writes=[t2b])
                        store(MIXIN[mrow(q, 1)[0]], mrow(q, 1)[1], tile, t2, t2b, ("MIXIN", mrow(q, 1), ti))
                    return f

                def epi_C(q):
                    def f(ti, tile, pss):
                        (px, pxb), (pb, pbb), (pc, pcb) = pss
                        n = tile[1]
                        t, tb = ntmp()
                        kb.op("act", lambda e: e.activation(out=t[:, 0:n], in_=pc[:, 0:n], func=AF.Copy), reads=[pcb], writes=[tb])
                        t2, t2b = ntmp()
                        kb.op("dve", lambda e: e.tensor_tensor(out=t2[:, 0:n], in0=px[:, 0:n], in1=t[:, 0:n], op=ALU.mult), reads=[pxb, tb], writes=[t2b])
                        store(MIXIN[mrow(q, 2)[0]], mrow(q, 2)[1], tile, t2, t2b, ("MIXIN", mrow(q, 2), ti))
                        t3, t3b = ntmp()
                        kb.op("act", lambda e: e.activation(out=t3[:, 0:n], in_=pb[:, 0:n], func=AF.Copy), reads=[pbb], writes=[t3b])
                        store(GBd, q * 128, tile, t3, t3b, ("GB", q, ti))
                    return f
                I0 = c.IN0
                jobs = []
                for q in range(WC):
                    jobs.append(([(I0 + q * 128, 0, KC)], epi_copy(MIXIN[mrow(q, 0)[0]], mrow(q, 0)[1], ("MIXIN", 0, mrow(q, 0)[0]))))
                    jobs.append(([(I0 + W + q * 128, 0, KC), (I0 + 2 * W + q * 128, 0, KC)], epi_B(q)))
                    jobs.append(([(I0 + 3 * W + q * 128, 0, KC), (I0 + 4 * W + q * 128, 0, KC), (I0 + 5 * W + q * 128, 0, KC)], epi_C(q)))
                    jobs.append(([(I0 + 6 * W + q * 128, 0, KC)], epi_copy(MIXIN[mrow(q, 3)[0]], mrow(q, 3)[1], ("MIXIN", 3, mrow(q, 3)[0]))))
                jobs1 = jobs
                jobs2 = []
                for k in range(4):
                    for q in range(WC):
                        jobs2.append(([(I0 + 7 * W + k * W + q * 128, 0, KC)], epi_copy(Sd, k * W + q * 128, "S", AF.Silu)))
                for k in range(4):
                    for m in range(KC):
                        jobs2.append(([(I0 + 11 * W + k * D + m * 128, 0, KC)], epi_copy(Gd_, k * D + m * 128, "G", AF.Sigmoid)))
                for hf in range(2):
                    hb = c.hbase[hf]
                    tl = [(ti, pos, c.tiles[ti]) for pos, ti in enumerate(c.halves[hf])]
                    hbuild(tl, hb)
                    if KSTOP <= 3:
                        continue
                    act_fn = lambda kk, off, n, hb=hb: act[:, kk, off - hb:off - hb + n]
                    run_gemm(l, jobs1, act_fn, actb, tl, wsl_, wbf, stg)
                    if KSTOP <= 4:
                        continue
                    run_gemm(l, jobs2, act_fn, actb, tl, wsl_, wbf, stg)
                if KSTOP <= 3:
                    kb.barrier(scr[:, 0:1])
                    return nc
                kb.barrier(scr[:, 0:1])
                for h in range(2):
                    kb.coll(MIXIN[h], MIXG[h], grp8, writes=[B("MIXG")])
                kb.barrier(scr[:, 0:1])
            if KSTOP <= 5:
                kb.barrier(scr[:, 0:1])
                return nc
            with ExitStack() as ph:
                Fb = [sb(f"F{i}", [128, LB], F32, ph) for i in range(4)]
                FB = [B(("F", i)) for i in range(4)]
                cand = [sb(f"cand{i}", [128, 8, 256], F32, ph) for i in range(2)]
                gst = [sb(f"gst{i}", [128, 2 * CH], F32, ph) for i in range(2)]
                GC = sb("GC", [128, max(NTC, 1), 2 * CH], F32, ph)
                NZ = 2048
                gi = sb("gi", [128, NZ], F32, ph)
                zs = sb("zs", [128, NZ], F32, ph)
                ztmp = [sb(f"zt{i}", [128, 512], F32, ph) for i in range(2)]
                zi = sb("zi", [128, 2, 4, CH], F32, ph)
                cdi = [0]

                def load_seq(br, cq, dst, dstb):
                    kb.op("pool", lambda e: e.memset(dst[:], 0.0), writes=[dstb])
                    r0 = br * CH + cq * 128
                    for i in range(4):
                        pieces = [(o_, min(256, TL - o_), P1 + i * TL + o_) for o_ in range(0, TL, 256)] + [(TL, TC, P0 + i * TC)]
                        for (o_, n_, d0) in pieces:
                            k = cdi[0]
                            cdi[0] = (k + 1) % 2
                            for bp in range(2):
                                for h in range(2):
                                    blk = MIXG[h][(bp * 4 + i) * 8 * CH:(bp * 4 + i + 1) * 8 * CH, :].rearrange("(j r) t -> r j t", j=2)
                                    kb.dma("sp", cand[k][:, bp * 4 + 2 * h:bp * 4 + 2 * h + 2, 0:n_], blk[r0:r0 + 128, :, o_:o_ + n_],
                                           reads=[B("MIXG")], writes=[B(("cand", k, bp, h))])
                            for j in range(8):
                                cb = B(("cand", k, j // 4, (j % 4) // 2))
                                if j == 0:
                                    kb.op("dve", lambda e: e.tensor_scalar(out=dst[:, d0:d0 + n_], in0=cand[k][:, 0, 0:n_], scalar1=sel[:, 0:1], scalar2=None, op0=ALU.mult),
                                          reads=[cb, B("cst")], writes=[dstb])
                                else:
                                    kb.op("dve", lambda e: e.scalar_tensor_tensor(out=dst[:, d0:d0 + n_], in0=cand[k][:, j, 0:n_], scalar=sel[:, j:j + 1],
                                                                                   in1=dst[:, d0:d0 + n_], op0=ALU.mult, op1=ALU.add),
                                          reads=[cb, B("cst"), dstb], writes=[dstb])

                def store_seq(br, cq, src, srcb):
                    for i in range(4):
                        r0 = (i % 2) * 4 * CH + br * CH + cq * 128
                        kb.dma("sp", MOUT[i // 2][r0:r0 + 128, 0:TL], src[:, P1 + i * TL:P1 + (i + 1) * TL], reads=[srcb], writes=[B(("MOUT", i, r0, 0))])
                        kb.dma("sp", MOUT[i // 2][r0:r0 + 128, TL:T], src[:, P0 + i * TC:P0 + (i + 1) * TC], reads=[srcb], writes=[B(("MOUT", i, r0, 1))])
                a0, b0 = PAD, LB - PAD
                Lc = b0 - a0

                def conv(eng, dst, dstb, src, srcb, wcol, ntap, padl, bias_ap):
                    for k in range(ntap):
                        sft = k - padl
                        wap = pp[:, wcol + k:wcol + k + 1]
                        if k == 0:
                            if bias_ap is not None:
                                kb.op(eng, lambda e: e.tensor_scalar(out=dst[:, a0:b0], in0=src[:, a0 + sft:b0 + sft], scalar1=wap, scalar2=bias_ap, op0=ALU.mult, op1=ALU.add),
                                      reads=[srcb, B("pp")], writes=[dstb])
                            else:
                                kb.op(eng, lambda e: e.tensor_scalar(out=dst[:, a0:b0], in0=src[:, a0 + sft:b0 + sft], scalar1=wap, scalar2=None, op0=ALU.mult),
                                      reads=[srcb, B("pp")], writes=[dstb])
                        else:
                            kb.op(eng, lambda e: e.scalar_tensor_tensor(out=dst[:, a0:b0], in0=src[:, a0 + sft:b0 + sft], scalar=wap, in1=dst[:, a0:b0], op0=ALU.mult, op1=ALU.add),
                                  reads=[srcb, B("pp"), dstb], writes=[dstb])
                for cq in range(NCH):
                    XS, XSb = Fb[0], FB[0]
                    load_seq(0, cq, XS, XSb)
                    if KSTOP <= 5.1:
                        kb.barrier(scr[:, 0:1])
                        return nc
                    Us = [(Fb[1], FB[1]), (Fb[2], FB[2])]
                    for d in range(2):
                        pc = cq * 2 + d
                        conv("dve", Us[d][0], Us[d][1], XS, XSb, po("lcw") + pc * 4, 4, 3 if d == 0 else 0, pp[:, po("lcb") + pc:po("lcb") + pc + 1])
                    if KSTOP <= 5.2:
                        kb.barrier(scr[:, 0:1])
                        return nc
                    R, Rb = Fb[0], FB[0]
                    I_, Ib = Fb[3], FB[3]
                    Y, Yb = Us[0]
                    for d in range(2):
                        pc = cq * 2 + d
                        U, Ub = Us[d]
                        lam = pp[:, po("lam") + pc:po("lam") + pc + 1]
                        kb.op("act", lambda e: e.activation(out=c1t[:, pc, 0:1], in_=lam, func=AF.Exp, scale=-1.0), reads=[B("pp")], writes=[B("c1t")])
                        kb.op("dve", lambda e: e.tensor_scalar(out=c1t[:, pc, 0:1], in0=c1t[:, pc, 0:1], scalar1=1.0, scalar2=None, op0=ALU.add), reads=[B("c1t")], writes=[B("c1t")])
                        kb.op("act", lambda e: e.activation(out=c1t[:, pc, 0:1], in_=c1t[:, pc, 0:1], func=AF.Ln), reads=[B("c1t")], writes=[B("c1t")])
                        kb.op("dve", lambda e: e.tensor_scalar(out=c1t[:, pc, 1:2], in0=c1t[:, pc, 0:1], scalar1=-8.0, scalar2=None, op0=ALU.mult), reads=[B("c1t")], writes=[B("c1t")])
                        if KSTOP <= 5.25:
                            kb.barrier(scr[:, 0:1])
                            return nc
                        tls = [(t0, min(512, b0 - t0)) for t0 in range(a0, b0, 512)]
                        for g_i in range(0, len(tls), 3):
                            gt = tls[g_i:g_i + 3]
                            gsp = ((0, R, Rb, po("lba") + pc), (1, I_, Ib, po("lbx") + pc))
                            bankss = []
                            for gsel, dstt, dstbb, bcol in gsp:
                                banks = [nps() for _ in gt]
                                bankss.append(banks)
                                g0 = ((d * 2 + gsel) * NCH + cq) * 128
                                kb.mm([pb_ for _, pb_ in banks], [(ps_[:, 0:n], gm[:, g0:g0 + 128], U[:, t0:t0 + n], True, True) for (ps_, _), (t0, n) in zip(banks, gt)],
                                      reads=[B("gm"), Ub])
                            for (gsel, dstt, dstbb, bcol), banks in zip(gsp, bankss):
                                kb.grp("act", [(lambda e, ps_=ps_, t0=t0, n=n, dstt=dstt, bcol=bcol: e.activation(out=dstt[:, t0:t0 + n], in_=ps_[:, 0:n], func=AF.Sigmoid, bias=pp[:, bcol:bcol + 1]))
                                               for (ps_, _), (t0, n) in zip(banks, gt)],
                                       reads=[pb_ for _, pb_ in banks] + [B("pp")], writes=[dstbb])
                        if KSTOP <= 5.3:
                            kb.barrier(scr[:, 0:1])
                            return nc
                        kb.op("act", lambda e: e.activation(out=R[:, a0:b0], in_=R[:, a0:b0], func=AF.Exp, scale=c1t[:, pc, 1:2]), reads=[Rb, B("c1t")], writes=[Rb])
                        kb.op("dve", lambda e: e.tensor_tensor(out=I_[:, a0:b0], in0=I_[:, a0:b0], in1=U[:, a0:b0], op=ALU.mult), reads=[Ib, Ub], writes=[Ib])
                        kb.op("dve", lambda e: e.tensor_tensor(out=U[:, a0:b0], in0=R[:, a0:b0], in1=R[:, a0:b0], op=ALU.mult), reads=[Rb], writes=[Ub])
                        kb.op("dve", lambda e: e.tensor_scalar(out=U[:, a0:b0], in0=U[:, a0:b0], scalar1=-1.0, scalar2=1.0, op0=ALU.mult, op1=ALU.add), reads=[Ub], writes=[Ub])
                        kb.op("act", lambda e: e.activation(out=U[:, a0:b0], in_=U[:, a0:b0], func=AF.Sqrt), reads=[Ub], writes=[Ub])
                        kb.op("dve", lambda e: e.tensor_tensor(out=I_[:, a0:b0], in0=I_[:, a0:b0], in1=U[:, a0:b0], op=ALU.mult), reads=[Ib, Ub], writes=[Ib])
                        if KSTOP <= 5.4:
                            kb.barrier(scr[:, 0:1])
                            return nc
                        if d == 0:
                            kb.op("dve", lambda e: e.tensor_tensor_scan(out=U[:, P0:P0 + CTX], data0=R[:, P0:P0 + CTX], data1=I_[:, P0:P0 + CTX], initial=0.0, op0=ALU.mult, op1=ALU.add),
                                  reads=[Rb, Ib], writes=[Ub])
                            kb.op("dve", lambda e: e.tensor_tensor_scan(out=U[:, P1:P1 + SEQ], data0=R[:, P1:P1 + SEQ], data1=I_[:, P1:P1 + SEQ], initial=U[:, P0 + CTX - 1:P0 + CTX], op0=ALU.mult, op1=ALU.add),
                                  reads=[Rb, Ib, Ub], writes=[Ub])
                        else:
                            kb.op("dve", lambda e: e.tensor_tensor_scan(out=U[:, P0:P0 + CTX][:, ::-1], data0=R[:, P0:P0 + CTX][:, ::-1], data1=I_[:, P0:P0 + CTX][:, ::-1], initial=0.0, op0=ALU.mult, op1=ALU.add),
                                  reads=[Rb, Ib], writes=[Ub])
                            kb.op("dve", lambda e: e.tensor_tensor_scan(out=U[:, P1:P1 + SEQ][:, ::-1], data0=R[:, P1:P1 + SEQ][:, ::-1], data1=I_[:, P1:P1 + SEQ][:, ::-1], initial=U[:, P0:P0 + 1], op0=ALU.mult, op1=ALU.add),
                                  reads=[Rb, Ib, Ub], writes=[Ub])
                            kb.op("dve", lambda e: e.tensor_tensor(out=Y[:, P0:P0 + CTX], in0=Y[:, P0:P0 + CTX], in1=U[:, P0:P0 + CTX], op=ALU.add), reads=[Yb, Ub], writes=[Yb])
                            kb.op("dve", lambda e: e.tensor_tensor(out=Y[:, P1:P1 + SEQ], in0=Y[:, P1:P1 + SEQ], in1=U[:, P1:P1 + SEQ], op=ALU.add), reads=[Yb, Ub], writes=[Yb])
                    if KSTOP <= 5.5:
                        kb.barrier(scr[:, 0:1])
                        return nc
                    store_seq(0, cq, Y, Yb)
                if KSTOP <= 6:
                    kb.barrier(scr[:, 0:1])
                    return nc
                for cq in range(NCH):
                    load_seq(1, cq, Fb[0], FB[0])
                    conv("dve", Fb[1], FB[1], Fb[0], FB[0], po("cfw") + cq * 31, 31, 15, pp[:, po("cfb") + cq:po("cfb") + cq + 1])
                    store_seq(1, cq, Fb[1], FB[1])
                    load_seq(2, cq, Fb[2], FB[2])
                    conv("dve", Fb[3], FB[3], Fb[2], FB[2], po("scw") + cq * 3, 3, 1, None)
                    store_seq(2, cq, Fb[3], FB[3])
                if KSTOP <= 7:
                    kb.barrier(scr[:, 0:1])
                    return nc
                for cq in range(NCH):
                    load_seq(3, cq, Fb[cq], FB[cq])
                csm = cst[:, co("cs"):co("cs") + NCH * 2 * CH].rearrange("p (q n) -> p q n", q=NCH)
                gk = [0]

                def stage0(t0, writer):
                    ps, psb = nps()
                    kb.mm(psb, [(ps[:, 0:2 * CH], Fb[q][:, t0:t0 + 128], csm[:, q, :], q == 0, q == NCH - 1) for q in range(NCH)],
                          reads=[FB[q] for q in range(NCH)] + [B("cst")])
                    writer(ps, psb)
                for tc in range(NTC):
                    def wr(ps, psb, tc=tc):
                        kb.op("act", lambda e: e.activation(out=GC[:, tc, :], in_=ps[:, 0:2 * CH], func=AF.Copy), reads=[psb], writes=[B("GC")])
                    stage0(P0 + tc * 128, wr)
                for t1 in range(T1):
                    def wr(ps, psb, t1=t1):
                        k = gk[0]
                        gk[0] = (k + 1) % 2
                        kb.op("act", lambda e: e.activation(out=gst[k][:], in_=ps[:, 0:2 * CH], func=AF.Copy), reads=[psb], writes=[B(("gst", k))])
                        kb.dma("sp", FG[:, t1, :, :].rearrange("r t c -> t r c"), gst[k][:].rearrange("p (r c) -> p r c", r=2),
                               reads=[B(("gst", k))], writes=[B("FG")])
                    stage0(P1 + t1 * 128, wr)
                K1 = 2 * T1
                TS2 = NZ // CH
                l1 = cst[0:K1, co("l1"):co("l1") + K1]
                l2 = cst[0:K1, co("l2"):co("l2") + K1]
                for s0 in range(0, 128, TS2):
                    kb.dma("sp", gi[0:K1, :].rearrange("p (t c) -> p t c", c=CH), FG[:, :, s0:s0 + TS2, :].rearrange("r a t c -> (r a) t c"),
                           reads=[B("FG")], writes=[B("gi")])
                    for n0 in range(0, NZ, 512):
                        p1_, p1b = nps()
                        p2_, p2b = nps()
                        kb.mm(p1b, [(p1_[0:K1, :], l1, gi[0:K1, n0:n0 + 512], True, True)], reads=[B("gi"), B("cst")])
                        kb.mm(p2b, [(p2_[0:K1, :], l2, gi[0:K1, n0:n0 + 512], True, True)], reads=[B("gi"), B("cst")])
                        k = gk[0]
                        gk[0] = (k + 1) % 2
                        zb = B(("zt", k))
                        NT = 512 // CH
                        t2b_ = s0 + (n0 // CH)
                        kb.grp("dve", [(lambda e, tt=tt: e.tensor_scalar(out=ztmp[k][0:K1, tt * CH:(tt + 1) * CH], in0=p1_[0:K1, tt * CH:(tt + 1) * CH],
                                                                          scalar1=cst[0:K1, co("twc") + t2b_ + tt:co("twc") + t2b_ + tt + 1], scalar2=None, op0=ALU.mult)) for tt in range(NT)],
                               reads=[p1b, B("cst")], writes=[zb])
                        kb.grp("dve", [(lambda e, tt=tt: e.scalar_tensor_tensor(out=zs[0:K1, n0 + tt * CH:n0 + (tt + 1) * CH], in0=p2_[0:K1, tt * CH:(tt + 1) * CH],
                                                                                 scalar=cst[0:K1, co("tws") + t2b_ + tt:co("tws") + t2b_ + tt + 1],
                                                                                 in1=ztmp[k][0:K1, tt * CH:(tt + 1) * CH], op0=ALU.mult, op1=ALU.add)) for tt in range(NT)],
                               reads=[p2b, zb, B("cst")], writes=[B("zs")])
                    kb.dma("sp", FZ[:, :, s0:s0 + TS2, :].rearrange("r a t c -> (r a) t c"), zs[0:K1, :].rearrange("p (t c) -> p t c", c=CH),
                           reads=[B("zs")], writes=[B("FZ")])
                RES = [Fb[2 + q] if NCH > 1 else Fb[2] for q in range(NCH)]
                RESb = [FB[2 + q] if NCH > 1 else FB[2] for q in range(NCH)]
                KG = 4
                c128 = cst[:, co("c128"):co("c128") + 128]
                s128 = cst[:, co("s128"):co("s128") + 128]
                for kg in range(0, T1, KG):
                    for r_ in range(2):
                        kb.dma("sp", zi[:, r_, :, :], FZ[r_, kg:kg + KG, :, :].rearrange("k t c -> t k c"), reads=[B("FZ")], writes=[B("zi")])
                    for q in range(NCH):
                        ps, psb = nps()
                        items = []
                        for kl in range(KG):
                            items.append((ps[:, kl * 128:(kl + 1) * 128], zi[:, 0, kl, q * 128:(q + 1) * 128], c128, True, False))
                            items.append((ps[:, kl * 128:(kl + 1) * 128], zi[:, 1, kl, q * 128:(q + 1) * 128], s128, False, True))
                        kb.mm(psb, items, reads=[B("zi"), B("cst")])
                        outv = RES[q][:, P1:P1 + SEQ].rearrange("p (k2 k1) -> p k1 k2", k1=T1)[:, kg:kg + KG, :]
                        kb.op("act", lambda e: e.activation(out=outv, in_=ps[:, 0:KG * 128].rearrange("p (k n) -> p k n", k=KG), func=AF.Copy),
                              reads=[psb], writes=[RESb[q]])
                cxm = cst[:, co("cx"):co("cx") + NTC * 2 * CTX].rearrange("p (a r n) -> p a r n", a=NTC, r=2)
                for q in range(NCH):
                    for n0 in range(0, CTX, 512):
                        nn = min(512, CTX - n0)
                        ps, psb = nps()
                        items = []
                        for tc in range(NTC):
                            for r in range(2):
                                items.append((ps[:, 0:nn], GC[:, tc, r * CH + q * 128:r * CH + (q + 1) * 128], cxm[:, tc, r, n0:n0 + nn],
                                              tc == 0 and r == 0, tc == NTC - 1 and r == 1))
                        kb.mm(psb, items, reads=[B("GC"), B("cst")])
                        kb.op("act", lambda e: e.activation(out=RES[q][:, P0 + n0:P0 + n0 + nn], in_=ps[:, 0:nn], func=AF.Copy), reads=[psb], writes=[RESb[q]])
                    store_seq(3, q, RES[q], RESb[q])
                kb.barrier(scr[:, 0:1])
                for h in range(2):
                    kb.coll(MOUT[h], MOUTG[h], grp8, writes=[B("MOUTG")])
                kb.barrier(scr[:, 0:1])
            if KSTOP <= 8:
                kb.barrier(scr[:, 0:1])
                return nc
            with ExitStack() as ph:
                act = sb("act2", [128, KC, c.TA], BF16, ph)
                wbf = [B(("w", i)) for i in range(8)]
                tpi = [0]
                TMPh = [None]

                def ntmp2():
                    i = tpi[0]
                    tpi[0] = (i + 1) % 6
                    return TMPh[0][i], B(("tmp", i))
                actb = [B(("act", pos)) for pos in range(len(c.tiles))]
                for hf in range(2):
                    hb = c.hbase[hf]
                    tl = [(ti, pos, c.tiles[ti]) for pos, ti in enumerate(c.halves[hf])]
                    act_fn = lambda kk, off, n, hb=hb: act[:, kk, off - hb:off - hb + n]
                    with ExitStack() as p3:
                        TMPh[0] = [sb(f"tq{i}", [128, 512], F32, p3) for i in range(6)]
                        cnd = [sb(f"cn{i}", [128, 8, 512], F32, p3) for i in range(1)]
                        YB = sb("YB", [128, WC, 512], F32, p3)
                        YSQ = sb("YSQ", [128, WC, 512], F32, p3)
                        sld = [sb(f"sld{i}", [128, 512], F32, p3) for i in range(2)]
                        mean = sb("mean", [128, 512], F32, p3)
                        rstd = sb("rstd", [128, 512], F32, p3)
                        ci = [0]
                        si = [0]

                        def sel_load(k, q, tile, dst_ap, dstb):
                            off, n, v = tile
                            ch0 = q * 128
                            j = ch0 // CH
                            r0 = k * CH + (ch0 % CH)
                            kk = 0
                            for bp in range(2):
                                for h in range(2):
                                    blk = MOUTG[h][(bp * 4 + j) * 8 * CH:(bp * 4 + j + 1) * 8 * CH, :].rearrange("(i r) t -> r i t", i=2)
                                    kb.dma("sp", cnd[kk][:, bp * 4 + 2 * h:bp * 4 + 2 * h + 2, 0:n], blk[r0:r0 + 128, :, off:off + n],
                                           reads=[B("MOUTG")], writes=[B(("cn", kk, bp, h))])
                            for i in range(8):
                                cb = B(("cn", kk, i // 4, (i % 4) // 2))
                                if i == 0:
                                    kb.op("dve", lambda e: e.tensor_scalar(out=dst_ap, in0=cnd[kk][:, 0, 0:n], scalar1=sel[:, 0:1], scalar2=None, op0=ALU.mult),
                                          reads=[cb, B("cst")], writes=[dstb])
                                else:
                                    kb.op("dve", lambda e: e.scalar_tensor_tensor(out=dst_ap, in0=cnd[kk][:, i, 0:n], scalar=sel[:, i:i + 1], in1=dst_ap, op0=ALU.mult, op1=ALU.add),
                                          reads=[cb, B("cst"), dstb], writes=[dstb])

                        def load_rows(src, r0, tile, key):
                            off, n, v = tile
                            kk = si[0]
                            si[0] = (kk + 1) % 2
                            kb.dma("sp", sld[kk][:, 0:n], src[r0:r0 + 128, off:off + n], reads=[B(key)], writes=[B(("sld", kk))])
                            return sld[kk], B(("sld", kk))
                        for (ti, pos, tile) in tl:
                            off, n, v = tile
                            for k in (0, 3, 2):
                                for q in range(WC):
                                    t, tb = ntmp2()
                                    sel_load(k, q, tile, t[:, 0:n], tb)
                                    if k == 2:
                                        g_, gb_ = load_rows(GBd, q * 128, tile, ("GB", q, ti))
                                        kb.op("dve", lambda e: e.tensor_tensor(out=t[:, 0:n], in0=t[:, 0:n], in1=g_[:, 0:n], op=ALU.mult), reads=[tb, gb_], writes=[tb])
                                    s_, sb_ = load_rows(Sd, k * W + q * 128, tile, ("S", k * W + q * 128, ti))
                                    kb.op("dve", lambda e: e.tensor_tensor(out=act[:, k * WC + q, off - hb:off - hb + n], in0=t[:, 0:n], in1=s_[:, 0:n], op=ALU.mult),
                                          reads=[tb, sb_], writes=[actb[pos]])
                            for q in range(WC):
                                sel_load(1, q, tile, YB[:, q, 0:n], B("YB"))
                            ps1, ps1b = nps()
                            ps2, ps2b = nps()
                            kb.mm(ps1b, [(ps1[:, 0:n], onesW[:], YB[:, q, 0:n], q == 0, q == WC - 1) for q in range(WC)], reads=[B("YB"), B("ones")])
                            kb.op("act", lambda e: e.activation(out=YSQ[:, :, 0:n], in_=YB[:, :, 0:n], func=AF.Square), reads=[B("YB")], writes=[B("YSQ")])
                            kb.mm(ps2b, [(ps2[:, 0:n], onesW[:], YSQ[:, q, 0:n], q == 0, q == WC - 1) for q in range(WC)], reads=[B("YSQ"), B("ones")])
                            kb.op("dve", lambda e: e.tensor_copy(out=mean[:, 0:n], in_=ps1[:, 0:n]), reads=[ps1b], writes=[B("mean")])
                            kb.op("dve", lambda e: e.tensor_tensor(out=rstd[:, 0:n], in0=mean[:, 0:n], in1=mean[:, 0:n], op=ALU.mult), reads=[B("mean")], writes=[B("rstd")])
                            kb.op("dve", lambda e: e.scalar_tensor_tensor(out=rstd[:, 0:n], in0=ps2[:, 0:n], scalar=1.0, in1=rstd[:, 0:n], op0=ALU.mult, op1=ALU.subtract),
                                  reads=[ps2b, B("rstd")], writes=[B("rstd")])
                            kb.op("act", lambda e: e.activation(out=rstd[:, 0:n], in_=rstd[:, 0:n], func=AF.Sqrt, bias=epsc[:, 0:1]), reads=[B("rstd"), B("ones")], writes=[B("rstd")])
                            kb.op("dve", lambda e: e.reciprocal(out=rstd[:, 0:n], in_=rstd[:, 0:n]), reads=[B("rstd")], writes=[B("rstd")])
                            for q in range(WC):
                                t, tb = ntmp2()
                                kb.op("dve", lambda e: e.tensor_tensor(out=t[:, 0:n], in0=YB[:, q, 0:n], in1=mean[:, 0:n], op=ALU.subtract), reads=[B("YB"), B("mean")], writes=[tb])
                                kb.op("dve", lambda e: e.tensor_tensor(out=t[:, 0:n], in0=t[:, 0:n], in1=rstd[:, 0:n], op=ALU.mult), reads=[tb, B("rstd")], writes=[tb])
                                kb.op("act", lambda e: e.activation(out=t[:, 0:n], in_=t[:, 0:n], func=AF.Silu, scale=pp[:, po("lng") + q:po("lng") + q + 1],
                                                                     bias=pp[:, po("lnb") + q:po("lnb") + q + 1]), reads=[tb, B("pp")], writes=[tb])
                                s_, sb_ = load_rows(Sd, W + q * 128, tile, ("S", W + q * 128, ti))
                                kb.op("dve", lambda e: e.tensor_tensor(out=act[:, WC + q, off - hb:off - hb + n], in0=t[:, 0:n], in1=s_[:, 0:n], op=ALU.mult),
                                      reads=[tb, sb_], writes=[actb[pos]])
                        kb.barrier(scr[:, 0:1])
                    if KSTOP <= 9:
                        kb.barrier(scr[:, 0:1])
                        return nc
                    with ExitStack() as p4:
                        TMPh[0] = [sb(f"tr{i}", [128, 512], F32, p4) for i in range(6)]
                        wsl_ = [sb(f"vs{i}", [128, WC, 128], BF16, p4) for i in range(8)]
                        stg = [sb(f"sg{i}", [128, WC, 128], F32, p4) for i in range(2)]
                        g4 = [sb(f"g4{i}", [128, 4, 512], F32, p4) for i in range(2)]
                        mo = [sb(f"mo{i}", [128, 512], BF16, p4) for i in range(2)]
                        gi_ = [0]

                        def epi_merge(m):
                            def f(ti, tile, pss):
                                off, n, v = tile
                                kk = gi_[0]
                                gi_[0] = (kk + 1) % 2
                                gb = B(("g4", kk))
                                kb.dma("sp", g4[kk][:, :, 0:n], Gd_.rearrange("(k d) t -> d k t", k=4)[m * 128:(m + 1) * 128, :, off:off + n],
                                       reads=[B(("G", k * D + m * 128, ti)) for k in range(4)], writes=[gb])
                                t, tb = ntmp2()
                                t2, t2b = ntmp2()
                                for k in range(4):
                                    ps, psb = pss[k]
                                    if k == 0:
                                        kb.op("dve", lambda e: e.tensor_tensor(out=t[:, 0:n], in0=ps[:, 0:n], in1=g4[kk][:, 0, 0:n], op=ALU.mult), reads=[psb, gb], writes=[tb])
                                    else:
                                        kb.op("dve", lambda e: e.tensor_tensor(out=t2[:, 0:n], in0=ps[:, 0:n], in1=g4[kk][:, k, 0:n], op=ALU.mult), reads=[psb, gb], writes=[t2b])
                                        if k < 3:
                                            kb.op("dve", lambda e: e.tensor_tensor(out=t[:, 0:n], in0=t[:, 0:n], in1=t2[:, 0:n], op=ALU.add), reads=[tb, t2b], writes=[tb])
                                        else:
                                            kb.op("dve", lambda e: e.tensor_tensor(out=mo[kk][:, 0:n], in0=t[:, 0:n], in1=t2[:, 0:n], op=ALU.add), reads=[tb, t2b], writes=[B(("mo", kk))])
                                kb.dma("sp", MT[m * 128:(m + 1) * 128, off:off + n], mo[kk][:, 0:n], reads=[B(("mo", kk))], writes=[B(("MT", ti))])
                            return f
                        jobs = [([(c.BR0 + m * 128, k * WC, (k + 1) * WC) for k in range(4)], epi_merge(m)) for m in range(KC)]
                        run_gemm(l, jobs, act_fn, actb, tl, wsl_, wbf, stg)
                        kb.barrier(scr[:, 0:1])
                    if KSTOP <= 10:
                        kb.barrier(scr[:, 0:1])
                        return nc
                    with ExitStack() as p5:
                        TMPh[0] = [sb(f"ts{i}", [128, 512], F32, p5) for i in range(6)]
                        wsl_ = [sb(f"vb{i}", [128, KC, 128], BF16, p5) for i in range(3)]
                        stg = [sb(f"sh{i}", [128, KC, 128], F32, p5) for i in range(2)]
                        for (ti, pos, (off, n, v)) in tl:
                            for hs in range(0, KC, 8):
                                kb.dma("sp", act[:, hs:hs + 8, off - hb:off - hb + n], MT.rearrange("(kc p) t -> p kc t", p=128)[:, hs:hs + 8, off:off + n], reads=[B(("MT", ti))], writes=[actb[pos]])

                        def epi_res(m):
                            def f(ti, tile, pss):
                                off, n, v = tile
                                ps, psb = pss[0]
                                t, tb = ntmp2()
                                kb.dma("sp", t[:, 0:n], XT[m * 128:(m + 1) * 128, off:off + n], reads=[B(("XT", ti))], writes=[tb])
                                t2, t2b = ntmp2()
                                kb.op("dve", lambda e: e.scalar_tensor_tensor(out=t2[:, 0:n], in0=ps[:, 0:n], scalar=modT[:, 2 * KC + m, v:v + 1], in1=t[:, 0:n], op0=ALU.mult, op1=ALU.add),
                                      reads=[psb, tb, B("modT")], writes=[t2b])
                                kb.dma("sp", XT[m * 128:(m + 1) * 128, off:off + n], t2[:, 0:n], reads=[t2b], writes=[B(("XT", ti))])
                            return f
                        jobs = [([(c.OUT0 + m * 128, 0, KC)], epi_res(m)) for m in range(KC)]
                        run_gemm(l, jobs, act_fn, actb, tl, wsl_, wbf[:3], stg)
                        kb.barrier(scr[:, 0:1])
                kb.barrier(scr[:, 0:1])
        with ExitStack() as p1:
            NS = 128
            xt = sb("xtf", [128, KC, NS], F32, p1)
            sqb = sb("sqbf", [128, KC, NS], F32, p1)
            rst = sb("rstf", [128, NS], F32, p1)
            fing = cst[:, co("fing"):co("fing") + KC]
            for ti, (off, n, v) in enumerate(c.tiles[:-1]):
                for so in range(0, n, NS):
                    ns = min(NS, n - so)
                    o0 = off + so
                    for hs in range(0, KC, 8):
                        kb.dma("sp", xt[:, hs:hs + 8, 0:ns], XT.rearrange("(kc p) t -> p kc t", p=128)[:, hs:hs + 8, o0:o0 + ns], reads=[B(("XT", ti))], writes=[B("xt")])
                    ps, psb = nps()
                    kb.op("act", lambda e: e.activation(out=sqb[:, :, 0:ns], in_=xt[:, :, 0:ns], func=AF.Square), reads=[B("xt")], writes=[B("sqb")])
                    kb.mm(psb, [(ps[:, 0:ns], ones[:], sqb[:, kk, 0:ns], kk == 0, kk == KC - 1) for kk in range(KC)], reads=[B("sqb"), B("ones")])
                    kb.op("act", lambda e: e.activation(out=rst[:, 0:ns], in_=ps[:, 0:ns], func=AF.Sqrt, bias=epsc[:, 0:1]), reads=[psb, B("ones")], writes=[B("rst")])
                    kb.op("dve", lambda e: e.reciprocal(out=rst[:, 0:ns], in_=rst[:, 0:ns]), reads=[B("rst")], writes=[B("rst")])
                    kb.grp("dve", [(lambda e, kk=kk: e.scalar_tensor_tensor(out=sqb[:, kk, 0:ns], in0=xt[:, kk, 0:ns], scalar=fing[:, kk:kk + 1],
                                                                             in1=rst[:, 0:ns], op0=ALU.mult, op1=ALU.mult)) for kk in range(KC)],
                           reads=[B("xt"), B("rst"), B("cst")], writes=[B("sqb")])
                    for hs in range(0, KC, 8):
                        kb.dma("sp", yT.rearrange("(kc p) t -> p kc t", p=128)[:, hs:hs + 8, o0:o0 + ns], sqb[:, hs:hs + 8, 0:ns], reads=[B("sqb")], writes=[B(("yT", hs))])
        kb.barrier(scr[:, 0:1])
    return nc


def _pc(v):
    return np.ascontiguousarray(v.reshape(-1, 128).T)


def _consts(c):
    CH, T1, CTX, SEQ, NCH, NTC = c.CH, c.T1, c.CTX, c.SEQ, c.NCH, c.NTC
    cst = np.zeros((128, c.NCST), np.float64)
    ch = np.arange(CH)
    ang = 2 * np.pi * np.outer(ch, ch) / CH
    cs = np.concatenate([np.cos(ang), -np.sin(ang)], axis=1)
    cst[:, c.co["cs"]:c.co["cs"] + NCH * 2 * CH] = cs.reshape(NCH, 128, 2 * CH).transpose(1, 0, 2).reshape(128, -1)
    a1 = 2 * np.pi * np.outer(np.arange(T1), np.arange(T1)) / T1
    C, S = np.cos(a1), np.sin(a1)
    L1 = np.block([[C, -S], [S, C]])
    L2 = np.block([[-S, -C], [C, -S]])
    cst[0:2 * T1, c.co["l1"]:c.co["l1"] + 2 * T1] = L1
    cst[0:2 * T1, c.co["l2"]:c.co["l2"] + 2 * T1] = L2
    th = 2 * np.pi * np.outer(np.arange(T1), np.arange(128)) / SEQ
    cst[0:2 * T1, c.co["twc"]:c.co["twc"] + 128] = np.concatenate([np.cos(th), np.cos(th)], 0)
    cst[0:2 * T1, c.co["tws"]:c.co["tws"] + 128] = np.concatenate([np.sin(th), np.sin(th)], 0)
    a2 = 2 * np.pi * np.outer(np.arange(128), np.arange(128)) / 128
    sc = 1.0 / np.sqrt(SEQ * CH)
    cst[:, c.co["c128"]:c.co["c128"] + 128] = np.cos(a2) * sc
    cst[:, c.co["s128"]:c.co["s128"] + 128] = np.sin(a2) * sc
    ax = 2 * np.pi * np.outer(np.arange(CTX), np.arange(CTX)) / CTX
    scx = 1.0 / np.sqrt(CTX * CH)
    cx = np.stack([np.cos(ax) * scx, np.sin(ax) * scx], 1)
    cx = cx.reshape(NTC, 128, 2, CTX).transpose(1, 0, 2, 3).reshape(128, -1)
    cst[:, c.co["cx"]:c.co["cx"] + NTC * 2 * CTX] = cx
    return cst.astype(np.float32)


def kernel(x, c, ctx, c_ctx, norm_g, w_mod, b_mod, w_in, lru_conv_w, lru_conv_b, lru_wa, lru_ba,
           lru_wx, lru_bx, lru_lambda, cf_conv_w, cf_conv_b, cf_ln_g, cf_ln_b, sc_conv_w,
           w_branch, w_out, final_g):
    f = lambda a: np.asarray(a, dtype=np.float32)
    x, c_, ctx, c_ctx = f(x), f(c), f(ctx), f(c_ctx)
    Bn, SEQ, D = x.shape
    CTX = ctx.shape[1]
    DEPTH = norm_g.shape[0]
    cfg = Cfg(D, SEQ, CTX, DEPTH)
    g = cfg
    W, CH, NCH, KC, WC, TL, TC = g.W, g.CH, g.NCH, g.KC, g.WC, g.TL, g.TC
    nc = build(cfg)
    wsl = [np.empty((DEPTH * 8 * g.RS, g.WCOLS), np.float32) for _ in range(8)]
    for l in range(DEPTH):
        for (src, c0) in ((f(w_mod[l]), g.MOD0), (f(w_in[l]), g.IN0), (f(w_branch[l]).reshape(D, D), g.BR0), (f(w_out[l]), g.OUT0)):
            v = src.reshape(8, 8, g.RS, src.shape[1])
            for r in range(8):
                for sidx in range(8):
                    i = l * 8 + sidx
                    wsl[r][i * g.RS:(i + 1) * g.RS, c0:c0 + src.shape[1]] = v[sidx, r]
    cst_base = _consts(cfg)
    hd = 64
    in_maps = []
    for r in range(8):
        b, rk = r // 4, r % 4
        xT = np.concatenate([x[b, rk * TL:(rk + 1) * TL, :].T, ctx[b, rk * TC:(rk + 1) * TC, :].T], axis=1)
        pp = np.zeros((DEPTH, 128, g.NP), np.float32)
        gm = np.zeros((DEPTH, 128, 4 * NCH * 128), np.float32)
        chs = slice(rk * CH, (rk + 1) * CH)
        for l in range(DEPTH):
            P = pp[l]
            P[:, g.po["bmod"]:g.po["bmod"] + 3 * KC] = _pc(f(b_mod[l]))
            P[:, g.po["normg"]:g.po["normg"] + KC] = _pc(f(norm_g[l]))
            P[:, g.po["lng"]:g.po["lng"] + WC] = _pc(f(cf_ln_g[l]))
            P[:, g.po["lnb"]:g.po["lnb"] + WC] = _pc(f(cf_ln_b[l]))
            for cq in range(NCH):
                cc = slice(rk * CH + cq * 128, rk * CH + (cq + 1) * 128)
                for d in range(2):
                    pc = cq * 2 + d
                    P[:, g.po["lcw"] + pc * 4:g.po["lcw"] + pc * 4 + 4] = f(lru_conv_w[l][d])[:, cc].T
                    P[:, g.po["lcb"] + pc] = f(lru_conv_b[l][d])[cc]
                    P[:, g.po["lba"] + pc] = f(lru_ba[l][d])[cc]
                    P[:, g.po["lbx"] + pc] = f(lru_bx[l][d])[cc]
                    P[:, g.po["lam"] + pc] = f(lru_lambda[l][d])[cc]
                    for gsel, wsrc in ((0, lru_wa), (1, lru_wx)):
                        g0 = ((d * 2 + gsel) * NCH + cq) * 128
                        h0 = (rk * CH + cq * 128) // hd
                        for hh in range(128 // hd):
                            gm[l, hh * hd:(hh + 1) * hd, g0 + hh * hd:g0 + (hh + 1) * hd] = f(wsrc[l][d][h0 + hh])
                P[:, g.po["cfw"] + cq * 31:g.po["cfw"] + cq * 31 + 31] = f(cf_conv_w[l])[:, cc].T
                P[:, g.po["cfb"] + cq] = f(cf_conv_b[l])[cc]
                P[:, g.po["scw"] + cq * 3:g.po["scw"] + cq * 3 + 3] = f(sc_conv_w[l])[:, cc].T
        cst = cst_base.copy()
        cst[:, g.co["sel"] + r] = 1.0
        cst[:, g.co["fing"]:g.co["fing"] + KC] = _pc(f(final_g))
        cv = np.stack([_pc(c_[b]), _pc(c_ctx)], axis=2)
        cst[:, g.co["cvec"]:g.co["cvec"] + 2 * KC] = cv.reshape(128, -1)
        in_maps.append({"xT": np.ascontiguousarray(xT), "wsl": wsl[r], "pp": pp, "gm": gm, "cst": cst})
    res = run_bass_kernel_spmd(nc, in_maps, core_ids=list(range(8)))
    out = np.empty((Bn, SEQ, D), np.float32)
    for r in range(8):
        b, rk = r // 4, r % 4
        out[b, rk * TL:(rk + 1) * TL, :] = res.results[r]["yT"].T
    return out
```
